# Optimizing a Trainium2 kernel written in Bass

```python
import math
import jax, jax.numpy as jnp
from jax import lax
import numpy as np

D_MODEL = 2048
BATCH = 2
SEQ = 16384
DEPTH = 2

N_META = 16
CHUNK = 128
FRONT_PAD = CHUNK - N_META
D_FF = 5632
MACARON_SCALE = 0.5
RMS_EPS = 1e-6
NEG_BIG = -1e30

ML_HEADS = 8
ML_DQK = D_MODEL // 2 // ML_HEADS
ML_DV = D_MODEL // ML_HEADS
ML_QK = ML_HEADS * ML_DQK
ML_V = ML_HEADS * ML_DV
ML_CONV = 4
ML_IN = 2 * ML_QK + 2 * ML_V + 2 * ML_HEADS

RT_HEADS = D_MODEL // 256
RT_DQK = 256
RT_DV = 2 * RT_DQK
RT_QK = RT_HEADS * RT_DQK
RT_V = RT_HEADS * RT_DV
RT_IN = 2 * RT_QK + 2 * RT_V
ROPE_BASE = 10000.0

N_ML_LAYERS = (DEPTH + 1) // 2
N_RT_LAYERS = DEPTH // 2

kernel_name = "hybrid_mlstm_retention_macaron"


def rmsnorm(x, gain):
    xf = x.astype(jnp.float32)
    y = xf * lax.rsqrt(jnp.mean(xf * xf, axis=-1, keepdims=True) + RMS_EPS)
    return (y * gain.astype(jnp.float32)).astype(x.dtype)


def head_rmsnorm(h, gain, n_heads):
    B, L, W = h.shape
    hh = h.reshape(B, L, n_heads, W // n_heads)
    hh = hh * lax.rsqrt(jnp.mean(hh * hh, axis=-1, keepdims=True) + RMS_EPS)
    return hh.reshape(B, L, W) * gain.astype(jnp.float32)


def swiglu(u, w13, w2):
    gate, up = jnp.split(u @ w13, 2, axis=-1)
    return (jax.nn.silu(gate) * up) @ w2


def causal_depthwise_conv(x, w, b):
    y = lax.conv_general_dilated(
        x, w[:, None, :].astype(x.dtype), window_strides=(1,),
        padding=[(w.shape[0] - 1, 0)], dimension_numbers=('NWC', 'WIO', 'NWC'),
        feature_group_count=x.shape[-1])
    return y + b.astype(x.dtype)


def to_chunks(a):
    B, L, H, d = a.shape
    return a.reshape(B, L // CHUNK, CHUNK, H, d).transpose(0, 3, 1, 2, 4)


def from_chunks(hs):
    NC, B, H, T, d = hs.shape
    return hs.transpose(1, 0, 3, 2, 4).reshape(B, NC * T, H * d)


def rope(x, positions):
    d = x.shape[-1]
    inv_freq = ROPE_BASE ** (-jnp.arange(0, d, 2, dtype=jnp.float32) / d)
    ang = positions[:, None] * inv_freq[None, :]
    cos = jnp.cos(ang)[None, :, None, :]
    sin = jnp.sin(ang)[None, :, None, :]
    x1, x2 = jnp.split(x, 2, axis=-1)
    return jnp.concatenate([x1 * cos - x2 * sin, x1 * sin + x2 * cos], axis=-1)


def mlstm_mixer(u, valid, w_in, b_gates, conv_w, conv_b, head_gain, w_out):
    B, L, _ = u.shape
    f32 = jnp.float32
    proj = u @ w_in
    qk_pre = proj[..., :2 * ML_QK]
    v = proj[..., 2 * ML_QK:2 * ML_QK + ML_V]
    o_pre = proj[..., 2 * ML_QK + ML_V:2 * ML_QK + 2 * ML_V]
    g_pre = proj[..., 2 * ML_QK + 2 * ML_V:]
    qk = jax.nn.silu(causal_depthwise_conv(qk_pre, conv_w, conv_b))
    q = qk[..., :ML_QK]
    k = qk[..., ML_QK:] * (ML_DQK ** -0.5)
    gates = g_pre.astype(f32) + b_gates.astype(f32)
    vmask = valid[:, None]
    li = jnp.where(vmask, gates[..., :ML_HEADS], NEG_BIG)
    lf = jnp.where(vmask, jax.nn.log_sigmoid(gates[..., ML_HEADS:]), 0.0)

    qc = to_chunks(q.astype(f32).reshape(B, L, ML_HEADS, ML_DQK))
    kc = to_chunks(k.astype(f32).reshape(B, L, ML_HEADS, ML_DQK))
    vc = to_chunks(v.astype(f32).reshape(B, L, ML_HEADS, ML_DV))
    lic = li.reshape(B, L // CHUNK, CHUNK, ML_HEADS).transpose(0, 3, 1, 2)
    lfc = lf.reshape(B, L // CHUNK, CHUNK, ML_HEADS).transpose(0, 3, 1, 2)
    xs = tuple(jnp.moveaxis(a, 2, 0) for a in (qc, kc, vc, lic, lfc))
    causal = jnp.tril(jnp.ones((CHUNK, CHUNK), dtype=bool))

    def step(carry, inp):
        C, n, m = carry
        q_, k_, v_, li_, lf_ = inp
        b = jnp.cumsum(lf_, axis=-1)
        g = b[..., -1]
        log_d = jnp.where(causal, b[..., :, None] - b[..., None, :] + li_[..., None, :], NEG_BIG)
        m_inter = b + m[..., None]
        m_t = jnp.maximum(m_inter, jnp.max(log_d, axis=-1))
        a_inter = jnp.exp(m_inter - m_t)
        s = jnp.einsum('bhtd,bhsd->bhts', q_, k_) * jnp.exp(log_d - m_t[..., None])
        num = jnp.einsum('bhts,bhsv->bhtv', s, v_) + a_inter[..., None] * jnp.einsum('bhtd,bhdv->bhtv', q_, C)
        den = jnp.sum(s, axis=-1) + a_inter * jnp.einsum('bhtd,bhd->bht', q_, n)
        h = num / jnp.maximum(jnp.abs(den), jnp.exp(-m_t))[..., None]
        log_w = g[..., None] - b + li_
        m_new = jnp.maximum(g + m, jnp.max(log_w, axis=-1))
        a_state = jnp.exp(g + m - m_new)
        kw = k_ * jnp.exp(log_w - m_new[..., None])[..., None]
        C_new = a_state[..., None, None] * C + jnp.einsum('bhsd,bhsv->bhdv', kw, v_)
        n_new = a_state[..., None] * n + jnp.sum(kw, axis=-2)
        return (C_new, n_new, m_new), h

    init = (jnp.zeros((B, ML_HEADS, ML_DQK, ML_DV), f32),
            jnp.zeros((B, ML_HEADS, ML_DQK), f32),
            jnp.zeros((B, ML_HEADS), f32))
    _, hs = lax.scan(step, init, xs)
    h = head_rmsnorm(from_chunks(hs), head_gain, ML_HEADS) * jax.nn.sigmoid(o_pre.astype(f32))
    return h.astype(u.dtype) @ w_out


def retention_mixer(u, positions, w_in, head_gain, w_out):
    B, L, _ = u.shape
    f32 = jnp.float32
    proj = u @ w_in
    q = proj[..., :RT_QK]
    k = proj[..., RT_QK:2 * RT_QK]
    v = proj[..., 2 * RT_QK:2 * RT_QK + RT_V]
    gate = proj[..., 2 * RT_QK + RT_V:]
    q = rope(q.astype(f32).reshape(B, L, RT_HEADS, RT_DQK), positions)
    k = rope(k.astype(f32).reshape(B, L, RT_HEADS, RT_DQK), positions) * (RT_DQK ** -0.5)
    qc, kc = to_chunks(q), to_chunks(k)
    vc = to_chunks(v.astype(f32).reshape(B, L, RT_HEADS, RT_DV))
    xs = tuple(jnp.moveaxis(a, 2, 0) for a in (qc, kc, vc))

    log_gamma = jnp.log(1.0 - 2.0 ** (-5.0 - jnp.arange(RT_HEADS, dtype=f32)))
    t = jnp.arange(CHUNK, dtype=f32)
    diff = t[:, None] - t[None, :]
    causal = diff >= 0
    decay_intra = jnp.where(causal, jnp.exp(jnp.where(causal, diff, 0.0)[None] * log_gamma[:, None, None]), 0.0)
    decay_q = jnp.exp((t + 1.0)[None, :] * log_gamma[:, None])[..., None]
    decay_k = jnp.exp((CHUNK - 1.0 - t)[None, :] * log_gamma[:, None])[..., None]
    decay_chunk = jnp.exp(CHUNK * log_gamma)[:, None, None]

    def step(R, inp):
        q_, k_, v_ = inp
        s = jnp.einsum('bhtd,bhsd->bhts', q_, k_) * decay_intra
        out = jnp.einsum('bhts,bhsv->bhtv', s, v_) + jnp.einsum('bhtd,bhdv->bhtv', q_, R) * decay_q
        R_new = decay_chunk * R + jnp.einsum('bhsd,bhsv->bhdv', k_ * decay_k, v_)
        return R_new, out

    _, outs = lax.scan(step, jnp.zeros((B, RT_HEADS, RT_DQK, RT_DV), f32), xs)
    y = head_rmsnorm(from_chunks(outs), head_gain, RT_HEADS) * jax.nn.silu(gate.astype(f32))
    return y.astype(u.dtype) @ w_out


def setup_inputs(seed: int = 0) -> dict:
    key = jax.random.key(seed)
    ks = jax.random.split(key, 24)
    f32 = jnp.float32

    def dense(k, shape, fan_in):
        return jax.random.normal(k, shape, f32) * (fan_in ** -0.5)

    def gain(k, shape):
        return 1.0 + 0.02 * jax.random.normal(k, shape, f32)

    i_bias = 0.01 * jax.random.normal(ks[14], (N_ML_LAYERS, ML_HEADS), f32)
    f_bias = jnp.linspace(3.0, 6.0, ML_HEADS, dtype=f32)[None, :] + 0.01 * jax.random.normal(ks[15], (N_ML_LAYERS, ML_HEADS), f32)
    return {
        "x": jax.random.normal(ks[0], (BATCH, SEQ, D_MODEL), f32),
        "meta_tokens": jax.random.normal(ks[1], (N_META, D_MODEL), f32),
        "ffn1_norm": gain(ks[2], (DEPTH, D_MODEL)),
        "ffn1_w13": dense(ks[3], (DEPTH, D_MODEL, 2 * D_FF), D_MODEL),
        "ffn1_w2": dense(ks[4], (DEPTH, D_FF, D_MODEL), D_FF),
        "mixer_norm": gain(ks[5], (DEPTH, D_MODEL)),
        "ffn2_norm": gain(ks[6], (DEPTH, D_MODEL)),
        "ffn2_w13": dense(ks[7], (DEPTH, D_MODEL, 2 * D_FF), D_MODEL),
        "ffn2_w2": dense(ks[8], (DEPTH, D_FF, D_MODEL), D_FF),
        "ml_w_in": dense(ks[9], (N_ML_LAYERS, D_MODEL, ML_IN), D_MODEL),
        "ml_b_gates": jnp.concatenate([i_bias, f_bias], axis=-1),
        "ml_conv_w": dense(ks[10], (N_ML_LAYERS, ML_CONV, 2 * ML_QK), ML_CONV),
        "ml_conv_b": 0.01 * jax.random.normal(ks[11], (N_ML_LAYERS, 2 * ML_QK), f32),
        "ml_head_norm": gain(ks[12], (N_ML_LAYERS, ML_V)),
        "ml_w_out": dense(ks[13], (N_ML_LAYERS, ML_V, D_MODEL), ML_V),
        "rt_w_in": dense(ks[16], (N_RT_LAYERS, D_MODEL, RT_IN), D_MODEL),
        "rt_head_norm": gain(ks[17], (N_RT_LAYERS, RT_V)),
        "rt_w_out": dense(ks[18], (N_RT_LAYERS, RT_V, D_MODEL), RT_V),
        "final_norm": gain(ks[19], (D_MODEL,)),
    }


def reference(x, meta_tokens, ffn1_norm, ffn1_w13, ffn1_w2, mixer_norm, ffn2_norm, ffn2_w13, ffn2_w2,
              ml_w_in, ml_b_gates, ml_conv_w, ml_conv_b, ml_head_norm, ml_w_out,
              rt_w_in, rt_head_norm, rt_w_out, final_norm):
    B = x.shape[0]
    pad = jnp.zeros((B, FRONT_PAD, D_MODEL), x.dtype)
    meta = jnp.broadcast_to(meta_tokens.astype(x.dtype)[None], (B, N_META, D_MODEL))
    h = jnp.concatenate([pad, meta, x], axis=1)
    L = h.shape[1]
    pidx = jnp.arange(L)
    valid = pidx >= FRONT_PAD
    positions = (pidx - FRONT_PAD).astype(jnp.float32)
    vmask = valid[:, None].astype(h.dtype)
    for layer in range(DEPTH):
        j = layer // 2
        h = h + MACARON_SCALE * swiglu(rmsnorm(h, ffn1_norm[layer]), ffn1_w13[layer], ffn1_w2[layer])
        u = rmsnorm(h, mixer_norm[layer]) * vmask
        if layer % 2 == 0:
            h = h + mlstm_mixer(u, valid, ml_w_in[j], ml_b_gates[j], ml_conv_w[j], ml_conv_b[j],
                                ml_head_norm[j], ml_w_out[j])
        else:
            h = h + retention_mixer(u, positions, rt_w_in[j], rt_head_norm[j], rt_w_out[j])
        h = h + MACARON_SCALE * swiglu(rmsnorm(h, ffn2_norm[layer]), ffn2_w13[layer], ffn2_w2[layer])
    return rmsnorm(h[:, FRONT_PAD + N_META:], final_norm)
```

```python
import contextlib
import math
import numpy as np
import concourse.bass as bass
import concourse.mybir as mybir
from concourse.bass_utils import run_bass_kernel_spmd

F32 = mybir.dt.float32
BF16 = mybir.dt.bfloat16
ALU = mybir.AluOpType
AF = mybir.ActivationFunctionType
EPS = 1e-6

CFG = dict(D=2048, DFF=5632, BATCH=2, SEQ=16384, NMETA=16, H=8, ML_DK=128, ML_DV=256, RT_DK=256, RT_DV=512,
           TT=256, TF=512, NSEG=4)


class Buf:
    __slots__ = ("name", "last_w", "reads", "dsem", "dcnt")

    def __init__(self, name):
        self.name = name
        self.last_w = None
        self.reads = []
        self.dsem = None
        self.dcnt = 0


class Sched:
    ENG = ("pe", "act", "dve", "pool", "sp")

    def __init__(self, nc, stack):
        self.nc = nc
        self.stack = stack
        self.ops = {e: [] for e in self.ENG}
        self.seen = {e: {} for e in self.ENG}
        self.esem = {e: stack.enter_context(nc.semaphore("es_" + e)) for e in self.ENG}
        self.nsem = 0

    def new_sem(self, name):
        self.nsem += 1
        return self.stack.enter_context(self.nc.semaphore("%s_%d" % (name, self.nsem)))

    def _need(self, eng, ev, waits):
        if ev is None:
            return
        key = ("e", ev[1]) if ev[0] == "e" else ("d", id(ev[1]))
        val = ev[2]
        if self.seen[eng].get(key, -1) >= val:
            return
        self.seen[eng][key] = val
        waits.append(ev)

    def _deps(self, eng, reads, writes):
        waits = []
        for b in reads:
            ev = b.last_w
            if ev is not None and not (ev[0] == "e" and ev[1] == eng and eng == "pe"):
                self._need(eng, ev, waits)
        for b in writes:
            ev = b.last_w
            if ev is not None and not (ev[0] == "e" and ev[1] == eng and eng == "pe"):
                self._need(eng, ev, waits)
            for rv in b.reads:
                if rv[0] == "e" and rv[1] == eng:
                    continue
                self._need(eng, rv, waits)
        return waits

    def op(self, eng, fn, reads=(), writes=()):
        waits = self._deps(eng, reads, writes)
        idx = len(self.ops[eng])
        self.ops[eng].append({"fn": fn, "waits": waits, "target": False, "dma": None})
        ev = ("e", eng, idx)
        for b in reads:
            b.reads.append(ev)
        for b in writes:
            b.last_w = ev
            b.reads = []
        return ev

    def dma(self, eng, fn, reads=(), writes=(), sem_buf=None):
        waits = self._deps(eng, reads, writes)
        sb = sem_buf if sem_buf is not None else writes[0]
        if sb.dsem is None:
            sb.dsem = self.new_sem("ds")
        sb.dcnt += 16
        ev = ("d", sb.dsem, sb.dcnt)
        self.ops[eng].append({"fn": fn, "waits": waits, "target": False, "dma": sb.dsem})
        for b in reads:
            b.reads.append(ev)
        for b in writes:
            b.last_w = ev
            b.reads = []
        return ev

    def wait_event(self, eng, ev):
        waits = []
        self._need(eng, ev, waits)
        if waits:
            self.ops[eng].append({"fn": None, "waits": waits, "target": False, "dma": None})

    def finalize(self):
        for e in self.ENG:
            for o in self.ops[e]:
                for w in o["waits"]:
                    if w[0] == "e":
                        self.ops[w[1]][w[2]]["target"] = True
        self.rank = {}
        for e in self.ENG:
            r = 0
            for i, o in enumerate(self.ops[e]):
                if o["target"]:
                    r += 1
                    self.rank[(e, i)] = r

    def replay(self, eng, handle):
        for o in self.ops[eng]:
            for w in o["waits"]:
                if w[0] == "e":
                    handle.wait_ge(self.esem[w[1]], self.rank[(w[1], w[2])])
                else:
                    handle.wait_ge(w[1], w[2])
            if o["fn"] is None:
                continue
            ins = o["fn"](handle)
            if o["dma"] is not None:
                ins.then_inc(o["dma"], 16)
            elif o["target"]:
                ins.then_inc(self.esem[eng], 1)

    def emit(self):
        self.finalize()
        with self.nc.Block() as block:
            @block.tensor
            def _(e):
                self.replay("pe", e)

            @block.scalar
            def _(e):
                self.replay("act", e)

            @block.vector
            def _(e):
                self.replay("dve", e)

            @block.gpsimd
            def _(e):
                self.replay("pool", e)

            @block.sync
            def _(e):
                self.replay("sp", e)


class Tl:
    def __init__(self, S, name, shape, dtype, psum=False):
        self.S = S
        self.name = name
        alloc = S.nc.psum_tensor if psum else S.nc.sbuf_tensor
        self.t = S.stack.enter_context(alloc(name, list(shape), dtype))
        self.bufs = {}

    def b(self, key=0):
        if key not in self.bufs:
            self.bufs[key] = Buf("%s_%s" % (self.name, str(key)))
        return self.bufs[key]


class Ctx:
    def __init__(self, nc, stack, cfg, TM, wcap):
        self.nc = nc
        self.S = Sched(nc, stack)
        self.cfg = cfg
        self.D = cfg["D"]
        self.KC = self.D // 128
        self.TM = TM
        S = self.S
        self.ones = Tl(S, "ones", [128, 128], F32)
        self.ones1 = Tl(S, "ones1", [128, 128], F32)
        self.epsc = Tl(S, "epsc", [128, 1], F32)
        self.sq = [Tl(S, "sq%d" % i, [128, TM], F32) for i in range(2)]
        self.rstd = Tl(S, "rstd", [128, TM], F32)
        self.uT = Tl(S, "uT", [128, self.KC, TM], BF16)
        self.ring = [Tl(S, "wr%d" % i, [128, wcap], BF16) for i in range(4)]
        self.ring_i = 0
        self.ps = [Tl(S, "ps%d" % i, [128, 512], F32, psum=True) for i in range(7)]
        self.psb = Tl(S, "psb", [128, 1024], BF16, psum=True)
        S.op("pool", lambda e: e.memset(self.ones.t[:, :], 1.0 / self.D), writes=[self.ones.b()])
        S.op("pool", lambda e: e.memset(self.ones1.t[:, :], 1.0), writes=[self.ones1.b()])
        S.op("pool", lambda e: e.memset(self.epsc.t[:, :], EPS), writes=[self.epsc.b()])
        self.nin = 0

    def load_w(self, dram_ap, n):
        w = self.ring[self.ring_i % len(self.ring)]
        self.ring_i += 1
        self.S.dma("pool", lambda e, w=w: e.dma_start(out=w.t[:, 0:n], in_=dram_ap), writes=[w.b()])
        return w

    def const_in(self, name, shape, dtype=F32):
        d = self.nc.dram_tensor(name, list(shape), dtype, kind="ExternalInput").ap()
        t = Tl(self.S, "c_" + name, shape, dtype)
        sl = tuple(slice(None) for _ in shape)
        self.S.dma("sp", lambda e: e.dma_start(out=t.t[sl], in_=d), writes=[t.b()])
        return t


def emit_norm(C, hT, gain, gcol0, T, mask=None):
    S, KC = C.S, C.KC
    ss = C.ps[0]
    for k in range(KC):
        sq = C.sq[k % 2]
        S.op("act", lambda e, k=k, sq=sq: e.activation(out=sq.t[:, :T], in_=hT.t[:, k, :T], func=AF.Square),
             reads=[hT.b(k)], writes=[sq.b()])
        S.op("pe", lambda e, k=k, sq=sq: e.matmul(ss.t[:, :T], lhsT=C.ones.t[:, :], rhs=sq.t[:, :T],
                                                   start=(k == 0), stop=(k == KC - 1)),
             reads=[sq.b(), C.ones.b()], writes=[ss.b()])
    rstd = C.rstd
    S.op("act", lambda e: e.activation(out=rstd.t[:, :T], in_=ss.t[:, :T], func=AF.Sqrt, bias=C.epsc.t[:, 0:1], scale=1.0),
         reads=[ss.b(), C.epsc.b()], writes=[rstd.b()])
    S.op("dve", lambda e: e.reciprocal(out=rstd.t[:, :T], in_=rstd.t[:, :T]), reads=[rstd.b()], writes=[rstd.b()])
    if mask is not None:
        S.op("dve", lambda e: e.tensor_tensor(out=rstd.t[:, :T], in0=rstd.t[:, :T], in1=mask.t[:, :T], op=ALU.mult),
             reads=[rstd.b(), mask.b()], writes=[rstd.b()])
    for k in range(KC):
        S.op("dve", lambda e, k=k: e.scalar_tensor_tensor(out=C.uT.t[:, k, :T], in0=hT.t[:, k, :T],
                                                           scalar=gain.t[:, gcol0 + k:gcol0 + k + 1],
                                                           in1=rstd.t[:, :T], op0=ALU.mult, op1=ALU.mult),
             reads=[hT.b(k), gain.b(), rstd.b()], writes=[C.uT.b(k)])


def emit_ffn(C, F, hT, gain, gcol0, w13_d, w2_d, T, scale=0.5):
    S, KC = C.S, C.KC
    FC = F["FC"]
    g = F["g"]
    uT = C.uT
    emit_norm(C, hT, gain, gcol0, T)
    for j in range(FC):
        w = C.load_w(w13_d[j], KC * 256)
        pg = C.ps[1 + (j % 2)]
        pu = C.ps[3 + (j % 2)]
        for k in range(KC):
            S.op("pe", lambda e, k=k, w=w, pg=pg: e.matmul(pg.t[:, :T], lhsT=w.t[:, k * 256:k * 256 + 128], rhs=uT.t[:, k, :T],
                                                           start=(k == 0), stop=(k == KC - 1)),
                 reads=[w.b(), uT.b(k)], writes=[pg.b()])
        for k in range(KC):
            S.op("pe", lambda e, k=k, w=w, pu=pu: e.matmul(pu.t[:, :T], lhsT=w.t[:, k * 256 + 128:k * 256 + 256], rhs=uT.t[:, k, :T],
                                                           start=(k == 0), stop=(k == KC - 1)),
                 reads=[w.b(), uT.b(k)], writes=[pu.b()])
        sg = F["sg"][j % 2]
        S.op("act", lambda e, sg=sg, pg=pg: e.activation(out=sg.t[:, :T], in_=pg.t[:, :T], func=AF.Silu),
             reads=[pg.b()], writes=[sg.b()])
        S.op("dve", lambda e, j=j, sg=sg, pu=pu: e.tensor_tensor(out=g.t[:, j, :T], in0=sg.t[:, :T], in1=pu.t[:, :T], op=ALU.mult),
             reads=[sg.b(), pu.b()], writes=[g.b(j)])
    for m in range(KC):
        w = C.load_w(w2_d[m], FC * 128)
        po = C.ps[5 + (m % 2)]
        for j in range(FC):
            S.op("pe", lambda e, j=j, w=w, po=po: e.matmul(po.t[:, :T], lhsT=w.t[:, j * 128:(j + 1) * 128], rhs=g.t[:, j, :T],
                                                           start=(j == 0), stop=(j == FC - 1)),
                 reads=[w.b(), g.b(j)], writes=[po.b()])
        S.op("dve", lambda e, m=m, po=po: e.scalar_tensor_tensor(out=hT.t[:, m, :T], in0=po.t[:, :T], scalar=float(scale),
                                                                 in1=hT.t[:, m, :T], op0=ALU.mult, op1=ALU.add),
             reads=[po.b(), hT.b(m)], writes=[hT.b(m)])


def tiles_of(cfg, TT):
    nreal = cfg["SEQ"] // cfg["NSEG"]
    out = [(0, 128, True)]
    off = 128
    while off < 128 + nreal:
        out.append((off, TT, False))
        off += TT
    return out


def build_ffn_launch(cfg, n_ffn, final_norm):
    D, DFF = cfg["D"], cfg["DFF"]
    KC, FC = D // 128, DFF // 128
    TF = cfg["TF"]
    NTOK = 128 + cfg["SEQ"] // cfg["NSEG"]
    nc = bass.Bass("TRN2", target_bir_lowering=False)
    h_in = nc.dram_tensor("h_in", [KC, 128, NTOK], F32, kind="ExternalInput").ap()
    h_out = nc.dram_tensor("h_out", [KC, 128, NTOK], F32, kind="ExternalOutput").ap()
    w13 = [nc.dram_tensor("w13_%d" % i, [FC, 128, KC * 256], F32, kind="ExternalInput").ap() for i in range(n_ffn)]
    w2 = [nc.dram_tensor("w2_%d" % i, [KC, 128, FC * 128], F32, kind="ExternalInput").ap() for i in range(n_ffn)]
    with contextlib.ExitStack() as stack:
        C = Ctx(nc, stack, cfg, TF, max(KC * 256, FC * 128))
        S = C.S
        gain = C.const_in("gains", [128, (n_ffn + 1) * KC])
        F = {"FC": FC, "g": Tl(S, "g", [128, FC, TF], BF16), "sg": [Tl(S, "sg%d" % i, [128, TF], F32) for i in range(2)]}
        hT = Tl(S, "hT", [128, KC, TF], F32)
        outb = Buf("outb")
        ev = None
        h_in_v = h_in.rearrange("k p t -> p k t")
        h_out_v = h_out.rearrange("k p t -> p k t")
        for (off, T, c0) in tiles_of(cfg, TF):
            S.dma("sp", lambda e, off=off, T=T: e.dma_start(out=hT.t[:, :, :T], in_=h_in_v[:, :, off:off + T]),
                  writes=[hT.b(k) for k in range(KC)])
            for i in range(n_ffn):
                emit_ffn(C, F, hT, gain, i * KC, w13[i], w2[i], T)
            src = hT
            if final_norm:
                emit_norm(C, hT, gain, n_ffn * KC, T)
                for k in range(KC):
                    S.op("dve", lambda e, k=k, T=T: e.scalar_tensor_tensor(out=hT.t[:, k, :T], in0=hT.t[:, k, :T],
                                                                         scalar=gain.t[:, n_ffn * KC + k:n_ffn * KC + k + 1],
                                                                         in1=C.rstd.t[:, :T], op0=ALU.mult, op1=ALU.mult),
                         reads=[hT.b(k), gain.b(), C.rstd.b()], writes=[hT.b(k)])
            ev = S.dma("sp", lambda e, off=off, T=T: e.dma_start(out=h_out_v[:, :, off:off + T], in_=hT.t[:, :, :T]),
                       reads=[hT.b(k) for k in range(KC)], sem_buf=outb)
        S.wait_event("sp", ev)
        S.emit()
    return nc


def build_mixer_launch(cfg, kind, full, dbgt=False):
    D, H = cfg["D"], cfg["H"]
    KC = D // 128
    TT = cfg["TT"]
    ml = kind == "ml"
    dk = cfg["ML_DK"] if ml else cfg["RT_DK"]
    dv = cfg["ML_DV"] if ml else cfg["RT_DV"]
    DKC = dk // 128
    dvx = dv + 1 if ml else dv
    VB = dv // 128
    KK = H * VB
    NTOK = 128 + cfg["SEQ"] // cfg["NSEG"]
    NSEG = cfg["NSEG"]
    SW = DKC * dvx
    nc = bass.Bass("TRN2", target_bir_lowering=False)
    dram = lambda n, s, k="ExternalInput": nc.dram_tensor(n, list(s), F32, kind=k).ap()
    h_in = dram("h_in", [KC, 128, NTOK])
    wq = dram("wq", [H, 128, KC * dk])
    wk = dram("wk", [H, 128, KC * dk])
    VH = dv // 256
    wv = dram("wv", [H * VH, 128, KC * 256])
    if full:
        wo = dram("wo", [H * VH, 128, KC * 256])
        wout = dram("wout", [KC, 128, KK * 128])
        st_all = dram("st_all", [NSEG - 1, 128, H * SW])
        h_out = dram("h_out", [KC, 128, NTOK], "ExternalOutput")
    else:
        st_out = dram("st_out", [128, H * SW], "ExternalOutput")
    if ml:
        G2 = 2 * H
        wg = dram("wg", [128, KC * G2])
        if not full:
            ng_out = dram("ng_out", [128, H], "ExternalOutput")
    else:
        cs_d = dram("cos_t", [128, NTOK])
        sn_d = dram("sin_t", [128, NTOK])
    with contextlib.ExitStack() as stack:
        C = Ctx(nc, stack, cfg, TT, max(KC * 256, KK * 128))
        S = C.S
        uT = C.uT
        NCH = TT // 128
        gain = C.const_in("gain", [128, KC])
        um = C.const_in("um", [128, 128])
        sel = C.const_in("sel", [128, 1])
        caus = C.const_in("caus", [128, 128])
        ident = C.const_in("ident", [128, 128], BF16)
        if ml:
            nsel = C.const_in("nsel", [128, 1])
            rowm = C.const_in("rowm", [128, 1])
            gbias = C.const_in("gbias", [128, G2])
            convw = C.const_in("convw", [128, 2 * H * 4])
            convb = C.const_in("convb", [128, 2 * H])
            lnsc = C.const_in("lnsc", [128, 1])
            halo = Tl(S, "halo", [128, 2 * H, 3], F32)
            S.op("dve", lambda e: e.memset(halo.t[:, :, :], 0.0), writes=[halo.b(i) for i in range(2 * H)])
            ngsum = Tl(S, "ngsum", [128, H], F32)
            S.op("dve", lambda e: e.memset(ngsum.t[:, :], 0.0), writes=[ngsum.b()])
        else:
            wtab = C.const_in("wtab", [128, H])
            ebtab = C.const_in("ebtab", [128, H])
            egtab = C.const_in("egtab", [128, H])
            cs = Tl(S, "cs", [128, TT], F32)
            sn = Tl(S, "sn", [128, TT], F32)
        if full:
            hgain_d = dram("hgain", [128, H * dv])
            hgain = Tl(S, "hgain_sb", [128, dv], F32)
            mseg = C.const_in("mseg", [128, NSEG - 1])
            if ml:
                ng_all = C.const_in("ng_all", [128, (NSEG - 1) * H])
            else:
                egseg = C.const_in("egseg", [128, H])
        St = Tl(S, "St", [128, H, SW], F32)
        S.op("dve", lambda e: e.memset(St.t[:, :, :], 0.0), writes=[St.b(h) for h in range(H)])
        Stb = Tl(S, "Stb", [128, SW], BF16)
        hT = Tl(S, "hT", [128, KC, TT], F32)
        qT = Tl(S, "qT", [128, DKC, TT], BF16)
        kT = Tl(S, "kT", [128, DKC, TT], BF16)
        ktok = Tl(S, "ktok", [128, NCH, dk], BF16)
        vw = Tl(S, "vw", [128, NCH, dvx], BF16)
        wsc = Tl(S, "wsc", [128, NCH, H], F32)
        ebt = Tl(S, "ebt", [128, NCH, H], F32)
        egt = Tl(S, "egt", [128, NCH, H], F32)
        tmpst = Tl(S, "tmpst", [128, SW], F32)
        if dbgt:
            dpst = Tl(S, "dpst", [128, SW], F32)
            dst0 = Tl(S, "dst0", [128, SW], F32)
        if ml:
            cbuf = [Tl(S, "cbuf%d" % i, [128, 3 + TT], F32) for i in range(2)]
            cacc = Tl(S, "cacc", [128, TT], F32)
            gts = Tl(S, "gts", [128, NCH, G2], F32)
            nlf = Tl(S, "nlf", [128, NCH, H], F32)
            nb = Tl(S, "nb", [128, NCH, H], F32)
            ngc = Tl(S, "ngc", [128, NCH, H], F32)
        else:
            rt1 = Tl(S, "rt1", [128, TT], F32)
            rt2 = Tl(S, "rt2", [128, TT], F32)
        if full:
            og = Tl(S, "og", [128, NCH, dv], F32)
            hout = Tl(S, "hout", [128, NCH, dv], F32)
            ytmp = Tl(S, "ytmp", [128, dv], F32)
            ybf = Tl(S, "ybf", [128, dv], BF16)
            yT = Tl(S, "yT", [128, KK, TT], BF16)
            P0T = Tl(S, "P0T", [128, 128], BF16)
            sm = Tl(S, "sm", [128, 8], F32)
            Ein = Tl(S, "Ein", [128, (NSEG - 1) * H], F32)
            if ml:
                S.op("act", lambda e: e.activation(out=Ein.t[:, :], in_=ng_all.t[:, :], func=AF.Exp, scale=-1.0),
                     reads=[ng_all.b()], writes=[Ein.b()])
            else:
                for cp in range(NSEG - 1):
                    S.op("dve", lambda e, cp=cp: e.tensor_copy(out=Ein.t[:, cp * H:(cp + 1) * H], in_=egseg.t[:, :]),
                         reads=[egseg.b()], writes=[Ein.b()])
            for cp in range(NSEG - 1):
                S.op("dve", lambda e, cp=cp: e.tensor_scalar(out=Ein.t[:, cp * H:(cp + 1) * H], in0=Ein.t[:, cp * H:(cp + 1) * H],
                                                             scalar1=-1.0, scalar2=mseg.t[:, cp:cp + 1], op0=ALU.add, op1=ALU.mult),
                     reads=[Ein.b(), mseg.b()], writes=[Ein.b()])
            S.op("dve", lambda e: e.tensor_scalar_add(out=Ein.t[:, :], in0=Ein.t[:, :], scalar1=1.0), reads=[Ein.b()], writes=[Ein.b()])
            Pacc = Tl(S, "Pacc", [128, SW], F32)
            Lt = Tl(S, "Lt", [128, SW], F32)
            st_all_v = st_all.rearrange("c p (h w) -> c p h w", h=H)
        outb = Buf("outb")
        ev_last = None
        h_in_v = h_in.rearrange("k p t -> p k t")
        if full:
            h_out_v = h_out.rearrange("k p t -> p k t")
        for (off, T, c0) in tiles_of(cfg, TT):
            nch = T // 128
            S.dma("sp", lambda e, off=off, T=T: e.dma_start(out=hT.t[:, :, :T], in_=h_in_v[:, :, off:off + T]),
                  writes=[hT.b(k) for k in range(KC)])
            emit_norm(C, hT, gain, 0, T, mask=um if c0 else None)
            if not ml:
                S.dma("sp", lambda e, off=off, T=T: e.dma_start(out=cs.t[:, :T], in_=cs_d[:, off:off + T]), writes=[cs.b()])
                S.dma("sp", lambda e, off=off, T=T: e.dma_start(out=sn.t[:, :T], in_=sn_d[:, off:off + T]), writes=[sn.b()])
            if ml:
                wgs = C.load_w(wg, KC * G2)
                for c in range(nch):
                    pgt = C.ps[0]
                    for k in range(KC):
                        S.op("pe", lambda e, k=k, c=c, wgs=wgs: e.matmul(pgt.t[:, 0:G2], lhsT=uT.t[:, k, c * 128:(c + 1) * 128],
                                                                          rhs=wgs.t[:, k * G2:(k + 1) * G2], start=(k == 0), stop=(k == KC - 1)),
                             reads=[uT.b(k), wgs.b()], writes=[pgt.b()])
                    S.op("dve", lambda e, c=c: e.tensor_tensor(out=gts.t[:, c, :], in0=pgt.t[:, 0:G2], in1=gbias.t[:, :], op=ALU.add),
                         reads=[pgt.b(), gbias.b()], writes=[gts.b()])
                S.op("act", lambda e, nch=nch: e.activation(out=nlf.t[:, :nch, :], in_=gts.t[:, :nch, H:2 * H], func=AF.Exp, scale=-1.0),
                     reads=[gts.b()], writes=[nlf.b()])
                S.op("act", lambda e, nch=nch: e.activation(out=nlf.t[:, :nch, :], in_=nlf.t[:, :nch, :], func=AF.Ln, bias=C.ones1.t[:, 0:1], scale=1.0),
                     reads=[nlf.b(), C.ones1.b()], writes=[nlf.b()])
                if c0:
                    S.op("dve", lambda e: e.tensor_scalar(out=nlf.t[:, 0, :], in0=nlf.t[:, 0, :], scalar1=rowm.t[:, 0:1], scalar2=None, op0=ALU.mult),
                         reads=[nlf.b(), rowm.b()], writes=[nlf.b()])
                for c in range(nch):
                    pcs = C.ps[0]
                    S.op("pe", lambda e, c=c: e.matmul(pcs.t[:, 0:H], lhsT=caus.t[:, :], rhs=nlf.t[:, c, :], start=True, stop=True),
                         reads=[caus.b(), nlf.b()], writes=[pcs.b()])
                    S.op("pe", lambda e, c=c: e.matmul(pcs.t[:, 64:64 + H], lhsT=C.ones1.t[:, :], rhs=nlf.t[:, c, :], start=True, stop=True),
                         reads=[C.ones1.b(), nlf.b()], writes=[pcs.b()])
                    S.op("dve", lambda e, c=c: e.tensor_copy(out=nb.t[:, c, :], in_=pcs.t[:, 0:H]), reads=[pcs.b()], writes=[nb.b()])
                    S.op("dve", lambda e, c=c: e.tensor_copy(out=ngc.t[:, c, :], in_=pcs.t[:, 64:64 + H]), reads=[pcs.b()], writes=[ngc.b()])
                    if not c0:
                        S.op("dve", lambda e, c=c: e.tensor_tensor(out=ngsum.t[:, :], in0=ngsum.t[:, :], in1=ngc.t[:, c, :], op=ALU.add),
                             reads=[ngsum.b(), ngc.b()], writes=[ngsum.b()])
                S.op("dve", lambda e, nch=nch: e.tensor_tensor(out=wsc.t[:, :nch, :], in0=gts.t[:, :nch, 0:H], in1=nb.t[:, :nch, :], op=ALU.add),
                     reads=[gts.b(), nb.b()], writes=[wsc.b()])
                S.op("act", lambda e, nch=nch: e.activation(out=wsc.t[:, :nch, :], in_=wsc.t[:, :nch, :], func=AF.Exp, bias=lnsc.t[:, 0:1], scale=1.0),
                     reads=[wsc.b(), lnsc.b()], writes=[wsc.b()])
                S.op("act", lambda e, nch=nch: e.activation(out=ebt.t[:, :nch, :], in_=nb.t[:, :nch, :], func=AF.Exp, scale=-1.0),
                     reads=[nb.b()], writes=[ebt.b()])
                S.op("act", lambda e, nch=nch: e.activation(out=egt.t[:, :nch, :], in_=ngc.t[:, :nch, :], func=AF.Exp, scale=-1.0),
                     reads=[ngc.b()], writes=[egt.b()])
                if c0:
                    S.op("dve", lambda e: e.tensor_scalar(out=wsc.t[:, 0, :], in0=wsc.t[:, 0, :], scalar1=rowm.t[:, 0:1], scalar2=sel.t[:, 0:1],
                                                          op0=ALU.mult, op1=ALU.mult), reads=[wsc.b(), rowm.b(), sel.b()], writes=[wsc.b()])
            else:
                for c in range(nch):
                    if c0:
                        S.op("dve", lambda e, c=c: e.tensor_scalar(out=wsc.t[:, c, :], in0=wtab.t[:, :], scalar1=sel.t[:, 0:1], scalar2=None, op0=ALU.mult),
                             reads=[wtab.b(), sel.b()], writes=[wsc.b()])
                    else:
                        S.op("dve", lambda e, c=c: e.tensor_copy(out=wsc.t[:, c, :], in_=wtab.t[:, :]), reads=[wtab.b()], writes=[wsc.b()])
                    S.op("dve", lambda e, c=c: e.tensor_copy(out=ebt.t[:, c, :], in_=ebtab.t[:, :]), reads=[ebtab.b()], writes=[ebt.b()])
                    S.op("dve", lambda e, c=c: e.tensor_copy(out=egt.t[:, c, :], in_=egtab.t[:, :]), reads=[egtab.b()], writes=[egt.b()])
            for h in range(H):
                for qi, (wd, dst) in enumerate(((wq, qT), (wk, kT))):
                    w = C.load_w(wd[h], KC * dk)
                    pqs = [C.ps[1 + dc] for dc in range(DKC)]
                    for dc in range(DKC):
                        for k in range(KC):
                            S.op("pe", lambda e, k=k, dc=dc, w=w, T=T: e.matmul(pqs[dc].t[:, :T], lhsT=w.t[:, k * dk + dc * 128:k * dk + dc * 128 + 128],
                                                                              rhs=uT.t[:, k, :T], start=(k == 0), stop=(k == KC - 1)),
                                 reads=[w.b(), uT.b(k)], writes=[pqs[dc].b()])
                    if ml:
                        fc = qi * H + h
                        cb = cbuf[qi]
                        S.op("act", lambda e, cb=cb, T=T: e.activation(out=cb.t[:, 3:3 + T], in_=pqs[0].t[:, :T], func=AF.Copy),
                             reads=[pqs[0].b()], writes=[cb.b()])
                        S.op("dve", lambda e, cb=cb, fc=fc: e.tensor_copy(out=cb.t[:, 0:3], in_=halo.t[:, fc, :]), reads=[halo.b(fc)], writes=[cb.b()])
                        S.op("dve", lambda e, cb=cb, fc=fc, T=T: e.tensor_scalar(out=cacc.t[:, :T], in0=cb.t[:, 3:3 + T], scalar1=convw.t[:, fc * 4 + 3:fc * 4 + 4],
                                                                                scalar2=convb.t[:, fc:fc + 1], op0=ALU.mult, op1=ALU.add),
                             reads=[cb.b(), convw.b(), convb.b()], writes=[cacc.b()])
                        for j in range(3):
                            S.op("dve", lambda e, cb=cb, fc=fc, j=j, T=T: e.scalar_tensor_tensor(out=cacc.t[:, :T], in0=cb.t[:, j:j + T],
                                                                                               scalar=convw.t[:, fc * 4 + j:fc * 4 + j + 1],
                                                                                               in1=cacc.t[:, :T], op0=ALU.mult, op1=ALU.add),
                                 reads=[cb.b(), convw.b(), cacc.b()], writes=[cacc.b()])
                        S.op("act", lambda e, dst=dst, T=T: e.activation(out=dst.t[:, 0, :T], in_=cacc.t[:, :T], func=AF.Silu),
                             reads=[cacc.b()], writes=[dst.b()])
                        if c0:
                            S.op("dve", lambda e, cb=cb, fc=fc: e.tensor_scalar(out=halo.t[:, fc, :], in0=cb.t[:, 3 + 125:3 + 128], scalar1=sel.t[:, 0:1],
                                                                                scalar2=None, op0=ALU.mult), reads=[cb.b(), sel.b()], writes=[halo.b(fc)])
                            S.op("dve", lambda e, cb=cb, fc=fc: e.scalar_tensor_tensor(out=halo.t[:, fc, :], in0=cb.t[:, 3 + 109:3 + 112], scalar=nsel.t[:, 0:1],
                                                                                       in1=halo.t[:, fc, :], op0=ALU.mult, op1=ALU.add),
                                 reads=[cb.b(), nsel.b(), halo.b(fc)], writes=[halo.b(fc)])
                        else:
                            S.op("dve", lambda e, cb=cb, fc=fc, T=T: e.tensor_copy(out=halo.t[:, fc, :], in_=cb.t[:, T:T + 3]), reads=[cb.b()], writes=[halo.b(fc)])
                    else:
                        S.op("dve", lambda e, T=T: e.tensor_tensor(out=rt1.t[:, :T], in0=pqs[0].t[:, :T], in1=cs.t[:, :T], op=ALU.mult),
                             reads=[pqs[0].b(), cs.b()], writes=[rt1.b()])
                        S.op("dve", lambda e, T=T: e.tensor_tensor(out=rt2.t[:, :T], in0=pqs[1].t[:, :T], in1=sn.t[:, :T], op=ALU.mult),
                             reads=[pqs[1].b(), sn.b()], writes=[rt2.b()])
                        S.op("dve", lambda e, dst=dst, T=T: e.tensor_tensor(out=dst.t[:, 0, :T], in0=rt1.t[:, :T], in1=rt2.t[:, :T], op=ALU.subtract),
                             reads=[rt1.b(), rt2.b()], writes=[dst.b()])
                        S.op("dve", lambda e, T=T: e.tensor_tensor(out=rt1.t[:, :T], in0=pqs[0].t[:, :T], in1=sn.t[:, :T], op=ALU.mult),
                             reads=[pqs[0].b(), sn.b()], writes=[rt1.b()])
                        S.op("dve", lambda e, T=T: e.tensor_tensor(out=rt2.t[:, :T], in0=pqs[1].t[:, :T], in1=cs.t[:, :T], op=ALU.mult),
                             reads=[pqs[1].b(), cs.b()], writes=[rt2.b()])
                        S.op("dve", lambda e, dst=dst, T=T: e.tensor_tensor(out=dst.t[:, 1, :T], in0=rt1.t[:, :T], in1=rt2.t[:, :T], op=ALU.add),
                             reads=[rt1.b(), rt2.b()], writes=[dst.b()])
                for c in range(nch):
                    for dc in range(DKC):
                        S.op("pe", lambda e, c=c, dc=dc: e.transpose(C.psb.t[:, dc * 128:(dc + 1) * 128], kT.t[:, dc, c * 128:(c + 1) * 128], ident.t[:, :]),
                             reads=[kT.b(), ident.b()], writes=[C.psb.b()])
                    S.op("act", lambda e, c=c: e.activation(out=ktok.t[:, c, :], in_=C.psb.t[:, 0:dk], func=AF.Copy),
                         reads=[C.psb.b()], writes=[ktok.b()])
                wvs = [C.load_w(wv[h * VH + vh], KC * 256) for vh in range(VH)]
                for c in range(nch):
                    pv = C.ps[3]
                    for vh in range(VH):
                        for k in range(KC):
                            S.op("pe", lambda e, k=k, c=c, vh=vh, wvs=wvs: e.matmul(pv.t[:, vh * 256:(vh + 1) * 256], lhsT=uT.t[:, k, c * 128:(c + 1) * 128],
                                                                         rhs=wvs[vh].t[:, k * 256:(k + 1) * 256], start=(k == 0), stop=(k == KC - 1)),
                                 reads=[uT.b(k), wvs[vh].b()], writes=[pv.b()])
                    S.op("dve", lambda e, c=c, h=h: e.tensor_scalar(out=vw.t[:, c, 0:dv], in0=pv.t[:, :dv], scalar1=wsc.t[:, c, h:h + 1], scalar2=None, op0=ALU.mult),
                         reads=[pv.b(), wsc.b()], writes=[vw.b()])
                    if ml:
                        S.op("dve", lambda e, c=c, h=h: e.tensor_copy(out=vw.t[:, c, dv:dv + 1], in_=wsc.t[:, c, h:h + 1]), reads=[wsc.b()], writes=[vw.b()])
                if full:
                    wos = [C.load_w(wo[h * VH + vh], KC * 256) for vh in range(VH)]
                    for c in range(nch):
                        po = C.ps[3]
                        for vh in range(VH):
                            for k in range(KC):
                                S.op("pe", lambda e, k=k, c=c, vh=vh, wos=wos: e.matmul(po.t[:, vh * 256:(vh + 1) * 256], lhsT=uT.t[:, k, c * 128:(c + 1) * 128],
                                                                             rhs=wos[vh].t[:, k * 256:(k + 1) * 256], start=(k == 0), stop=(k == KC - 1)),
                                     reads=[uT.b(k), wos[vh].b()], writes=[po.b()])
                        S.op("act", lambda e, c=c: e.activation(out=og.t[:, c, :], in_=po.t[:, :dv], func=(AF.Sigmoid if ml else AF.Silu)),
                             reads=[po.b()], writes=[og.b()])
                for c in range(nch):
                    csl = slice(c * 128, (c + 1) * 128)
                    if full:
                        S.op("act", lambda e, h=h: e.activation(out=Stb.t[:, :], in_=St.t[:, h, :], func=AF.Copy), reads=[St.b(h)], writes=[Stb.b()])
                        pS = C.ps[4]
                        for dc in range(DKC):
                            S.op("pe", lambda e, dc=dc, csl=csl: e.matmul(pS.t[:, 0:128], lhsT=kT.t[:, dc, csl], rhs=qT.t[:, dc, csl],
                                                                          start=(dc == 0), stop=(dc == DKC - 1)),
                                 reads=[kT.b(), qT.b()], writes=[pS.b()])
                        S.op("dve", lambda e: e.tensor_tensor(out=P0T.t[:, :], in0=pS.t[:, 0:128], in1=caus.t[:, :], op=ALU.mult),
                             reads=[pS.b(), caus.b()], writes=[P0T.b()])
                        pn = C.ps[5]
                        S.op("pe", lambda e, c=c: e.matmul(pn.t[:, :dvx], lhsT=P0T.t[:, :], rhs=vw.t[:, c, :dvx], start=True, stop=False),
                             reads=[P0T.b(), vw.b()], writes=[pn.b()])
                        for dc in range(DKC):
                            S.op("pe", lambda e, dc=dc, csl=csl: e.matmul(pn.t[:, :dvx], lhsT=qT.t[:, dc, csl], rhs=Stb.t[:, dc * dvx:(dc + 1) * dvx],
                                                                          start=False, stop=(dc == DKC - 1)),
                                 reads=[qT.b(), Stb.b()], writes=[pn.b()])
                        if ml:
                            S.op("act", lambda e, c=c, h=h: e.activation(out=sm.t[:, 0:1], in_=pn.t[:, dv:dv + 1], func=AF.Abs, scale=ebt.t[:, c, h:h + 1]),
                                 reads=[pn.b(), ebt.b()], writes=[sm.b()])
                            S.op("dve", lambda e: e.tensor_scalar_max(out=sm.t[:, 0:1], in0=sm.t[:, 0:1], scalar1=1.0), reads=[sm.b()], writes=[sm.b()])
                            S.op("dve", lambda e: e.reciprocal(out=sm.t[:, 0:1], in_=sm.t[:, 0:1]), reads=[sm.b()], writes=[sm.b()])
                            S.op("dve", lambda e, c=c, h=h: e.tensor_tensor(out=sm.t[:, 1:2], in0=sm.t[:, 0:1], in1=ebt.t[:, c, h:h + 1], op=ALU.mult),
                                 reads=[sm.b(), ebt.b()], writes=[sm.b()])
                            S.op("dve", lambda e, c=c: e.tensor_scalar(out=hout.t[:, c, :], in0=pn.t[:, :dv], scalar1=sm.t[:, 1:2], scalar2=None, op0=ALU.mult),
                                 reads=[pn.b(), sm.b()], writes=[hout.b()])
                        else:
                            S.op("dve", lambda e, c=c, h=h: e.tensor_scalar(out=hout.t[:, c, :], in0=pn.t[:, :dv], scalar1=ebt.t[:, c, h:h + 1], scalar2=None, op0=ALU.mult),
                                 reads=[pn.b(), ebt.b()], writes=[hout.b()])
                    for dc in range(DKC):
                        pst = C.ps[6] if dc == 0 else C.ps[0]
                        S.op("pe", lambda e, dc=dc, c=c, pst=pst: e.matmul(pst.t[:, :dvx], lhsT=ktok.t[:, c, dc * 128:(dc + 1) * 128], rhs=vw.t[:, c, :dvx],
                                                                           start=True, stop=True), reads=[ktok.b(), vw.b()], writes=[pst.b()])
                        if dbgt:
                            S.op("dve", lambda e, dc=dc, pst=pst: e.tensor_copy(out=dpst.t[:, dc * dvx:(dc + 1) * dvx], in_=pst.t[:, :dvx]),
                                 reads=[pst.b()], writes=[dpst.b()])
                            S.op("dve", lambda e, h=h: e.tensor_copy(out=dst0.t[:, :], in_=St.t[:, h, :]), reads=[St.b(h)], writes=[dst0.b()])
                        S.op("dve", lambda e, dc=dc, h=h, pst=pst: e.tensor_tensor(out=tmpst.t[:, dc * dvx:(dc + 1) * dvx], in0=pst.t[:, :dvx],
                                                                                  in1=St.t[:, h, dc * dvx:(dc + 1) * dvx], op=ALU.add),
                             reads=[pst.b(), St.b(h)], writes=[tmpst.b()])
                    S.op("dve", lambda e, c=c, h=h: e.tensor_scalar(out=St.t[:, h, :], in0=tmpst.t[:, :], scalar1=egt.t[:, c, h:h + 1], scalar2=None, op0=ALU.mult),
                         reads=[tmpst.b(), egt.b()], writes=[St.b(h)])
                if c0 and full:
                    S.op("dve", lambda e: e.memset(Pacc.t[:, :], 0.0), writes=[Pacc.b()])
                    for cp in range(NSEG - 1):
                        S.dma("sp", lambda e, cp=cp, h=h: e.dma_start(out=Lt.t[:, :], in_=st_all_v[cp, :, h, :]), writes=[Lt.b()])
                        S.op("dve", lambda e, cp=cp: e.tensor_scalar(out=Lt.t[:, :], in0=Lt.t[:, :], scalar1=mseg.t[:, cp:cp + 1], scalar2=None,
                                                                     op0=ALU.mult), reads=[Lt.b(), mseg.b()], writes=[Lt.b()])
                        S.op("dve", lambda e, cp=cp, h=h: e.scalar_tensor_tensor(out=Pacc.t[:, :], in0=Pacc.t[:, :],
                                                                                 scalar=Ein.t[:, cp * H + h:cp * H + h + 1],
                                                                                 in1=Lt.t[:, :], op0=ALU.mult, op1=ALU.add),
                             reads=[Pacc.b(), Ein.b(), Lt.b()], writes=[Pacc.b()])
                    S.op("dve", lambda e, h=h: e.tensor_tensor(out=St.t[:, h, :], in0=St.t[:, h, :], in1=Pacc.t[:, :], op=ALU.add),
                         reads=[St.b(h), Pacc.b()], writes=[St.b(h)])
                if full:
                    S.dma("sp", lambda e, h=h: e.dma_start(out=hgain.t[:, :], in_=hgain_d[:, h * dv:(h + 1) * dv]), writes=[hgain.b()])
                    for c in range(nch):
                        S.op("act", lambda e, c=c: e.activation(out=ytmp.t[:, :], in_=hout.t[:, c, :], func=AF.Square, accum_out=sm.t[:, 2:3]),
                             reads=[hout.b()], writes=[ytmp.b(), sm.b()])
                        S.op("act", lambda e: e.activation(out=sm.t[:, 3:4], in_=sm.t[:, 2:3], func=AF.Sqrt, bias=C.epsc.t[:, 0:1], scale=1.0 / dv),
                             reads=[sm.b(), C.epsc.b()], writes=[sm.b()])
                        S.op("dve", lambda e: e.reciprocal(out=sm.t[:, 3:4], in_=sm.t[:, 3:4]), reads=[sm.b()], writes=[sm.b()])
                        S.op("dve", lambda e, c=c, h=h: e.scalar_tensor_tensor(out=ytmp.t[:, :], in0=hout.t[:, c, :], scalar=sm.t[:, 3:4],
                                                                               in1=hgain.t[:, :], op0=ALU.mult, op1=ALU.mult),
                             reads=[hout.b(), sm.b(), hgain.b()], writes=[ytmp.b()])
                        S.op("dve", lambda e, c=c: e.tensor_tensor(out=ybf.t[:, :], in0=ytmp.t[:, :], in1=og.t[:, c, :], op=ALU.mult),
                             reads=[ytmp.b(), og.b()], writes=[ybf.b()])
                        for vb in range(VB):
                            S.op("pe", lambda e, vb=vb: e.transpose(C.psb.t[:, 512 + vb * 128:512 + (vb + 1) * 128], ybf.t[:, vb * 128:(vb + 1) * 128], ident.t[:, :]),
                                 reads=[ybf.b(), ident.b()], writes=[C.psb.b()])
                        for vb in range(VB):
                            S.op("act", lambda e, vb=vb, c=c, h=h: e.activation(out=yT.t[:, h * VB + vb, c * 128:(c + 1) * 128],
                                                                               in_=C.psb.t[:, 512 + vb * 128:512 + (vb + 1) * 128], func=AF.Copy),
                                 reads=[C.psb.b()], writes=[yT.b(h * VB + vb)])
            if dbgt:
                ti = off // 128
                for (nm, tl, dt) in (("St", St, F32), ("vwt", vw, BF16), ("wsct", wsc, F32), ("egtt", egt, F32)):
                    shp = list(tl.t.shape)
                    dd = nc.dram_tensor("dbg_%s_%d" % (nm, ti), shp, dt, kind="ExternalOutput").ap()
                    sl = tuple(slice(None) for _ in shp)
                    S.dma("sp", lambda e, dd=dd, tl=tl, sl=sl: e.dma_start(out=dd, in_=tl.t[sl]), reads=list(tl.bufs.values()), sem_buf=outb)
            if full:
                for m in range(KC):
                    w = C.load_w(wout[m], KK * 128)
                    po = C.ps[1 + (m % 2)]
                    for kk in range(KK):
                        S.op("pe", lambda e, kk=kk, w=w, po=po, T=T: e.matmul(po.t[:, :T], lhsT=w.t[:, kk * 128:(kk + 1) * 128], rhs=yT.t[:, kk, :T],
                                                                            start=(kk == 0), stop=(kk == KK - 1)),
                             reads=[w.b(), yT.b(kk)], writes=[po.b()])
                    S.op("dve", lambda e, m=m, po=po, T=T: e.tensor_tensor(out=hT.t[:, m, :T], in0=po.t[:, :T], in1=hT.t[:, m, :T], op=ALU.add),
                         reads=[po.b(), hT.b(m)], writes=[hT.b(m)])
                ev_last = S.dma("sp", lambda e, off=off, T=T: e.dma_start(out=h_out_v[:, :, off:off + T], in_=hT.t[:, :, :T]),
                                reads=[hT.b(k) for k in range(KC)], sem_buf=outb)
        if not full:
            ev_last = S.dma("sp", lambda e: e.dma_start(out=st_out.rearrange("p (h w) -> p h w", h=H), in_=St.t[:, :, :]),
                            reads=[St.b(h) for h in range(H)], sem_buf=outb)
            if ml:
                ev_last = S.dma("sp", lambda e: e.dma_start(out=ng_out, in_=ngsum.t[:, :]), reads=[ngsum.b()], sem_buf=outb)
        if dbgt:
            dl = [("wsc", wsc, F32), ("ebt", ebt, F32), ("egt", egt, F32), ("qT", qT, BF16), ("kT", kT, BF16), ("ktok", ktok, BF16),
                  ("vw", vw, BF16), ("tmpst", tmpst, F32), ("uT", uT, BF16), ("dpst", dpst, F32), ("dst0", dst0, F32)]
            if ml:
                dl += [("gts", gts, F32), ("nlf", nlf, F32), ("nb", nb, F32), ("ngc", ngc, F32), ("cacc", cacc, F32), ("cbuf0", cbuf[0], F32)]
            for (nm, tl, dt) in dl:
                shp = list(tl.t.shape)
                dd = nc.dram_tensor("dbg_" + nm, shp, dt, kind="ExternalOutput").ap()
                sl = tuple(slice(None) for _ in shp)
                ev_last = S.dma("sp", lambda e, dd=dd, tl=tl, sl=sl: e.dma_start(out=dd, in_=tl.t[sl]),
                                reads=list(tl.bufs.values()), sem_buf=outb)
        S.wait_event("sp", ev_last)
        S.emit()
    return nc


def _tile_k(W, KC):
    cols = W.shape[1]
    return np.ascontiguousarray(W.reshape(KC, 128, cols).transpose(1, 0, 2).reshape(128, KC * cols))


def _prep_ffn(w13, w2, D, DFF):
    KC, FC = D // 128, DFF // 128
    gate, up = w13[:, :DFF], w13[:, DFF:]
    a = np.empty((FC, 128, KC, 256), np.float32)
    gt = gate.reshape(KC, 128, FC, 128)
    ut = up.reshape(KC, 128, FC, 128)
    a[:, :, :, 0:128] = gt.transpose(2, 1, 0, 3)
    a[:, :, :, 128:256] = ut.transpose(2, 1, 0, 3)
    b = np.ascontiguousarray(w2.reshape(FC, 128, KC, 128).transpose(2, 1, 0, 3)).reshape(KC, 128, FC * 128)
    return a.reshape(FC, 128, KC * 256), b


def _gain_cols(g, KC):
    return np.ascontiguousarray(g.reshape(KC, 128).T)


def _run(nc, maps, ncores):
    res = run_bass_kernel_spmd(nc, maps, core_ids=list(range(ncores)))
    return res.results


def kernel(x, meta_tokens, ffn1_norm, ffn1_w13, ffn1_w2, mixer_norm, ffn2_norm, ffn2_w13, ffn2_w2,
           ml_w_in, ml_b_gates, ml_conv_w, ml_conv_b, ml_head_norm, ml_w_out,
           rt_w_in, rt_head_norm, rt_w_out, final_norm, cfg=None, dbg=None):
    cfg = cfg or CFG
    f = lambda a: np.asarray(a, dtype=np.float32)
    x, meta_tokens = f(x), f(meta_tokens)
    D, DFF, H = cfg["D"], cfg["DFF"], cfg["H"]
    KC = D // 128
    B, NSEG = cfg["BATCH"], cfg["NSEG"]
    NC = B * NSEG
    SEGL = cfg["SEQ"] // NSEG
    NTOK = 128 + SEGL
    NM = cfg["NMETA"]
    h = []
    for ci in range(NC):
        b, c = divmod(ci, NSEG)
        tok = np.zeros((NTOK, D), np.float32)
        if c > 0:
            tok[109:112] = x[b, c * SEGL - 3:c * SEGL]
        tok[128 - NM:128] = meta_tokens
        tok[128:] = x[b, c * SEGL:(c + 1) * SEGL]
        h.append(np.ascontiguousarray(tok.T.reshape(KC, 128, NTOK)))
    sel = [np.full((128, 1), 1.0 if ci % NSEG == 0 else 0.0, np.float32) for ci in range(NC)]
    nsel = [1.0 - s for s in sel]
    mseg = [np.broadcast_to(np.array([1.0 if cp < ci % NSEG else 0.0 for cp in range(NSEG - 1)], np.float32), (128, NSEG - 1)).copy() for ci in range(NC)]
    caus = np.triu(np.ones((128, 128), np.float32))
    import ml_dtypes
    ident = np.eye(128, dtype=np.float32).astype(ml_dtypes.bfloat16)
    colmask = lambda lo: np.broadcast_to((np.arange(128) >= lo).astype(np.float32), (128, 128)).copy()
    rowm = (np.arange(128) >= 128 - NM).astype(np.float32).reshape(128, 1)

    ffn_w = []
    for (w13, w2) in ((ffn1_w13[0], ffn1_w2[0]), (ffn2_w13[0], ffn2_w2[0]), (ffn1_w13[1], ffn1_w2[1]), (ffn2_w13[1], ffn2_w2[1])):
        ffn_w.append(_prep_ffn(f(w13), f(w2), D, DFF))
    gcat = lambda *gs: np.concatenate([_gain_cols(f(g), KC) for g in gs], axis=1)

    def ffn_launch(h, idxs, gains, final):
        nc = build_ffn_launch(cfg, len(idxs), final)
        maps = []
        for ci in range(NC):
            m = {"h_in": h[ci], "gains": gains}
            for i, wi in enumerate(idxs):
                m["w13_%d" % i] = ffn_w[wi][0]
                m["w2_%d" % i] = ffn_w[wi][1]
            maps.append(m)
        return [r["h_out"] for r in _run(nc, maps, NC)]

    def mixer_maps(kind, full, h, extra):
        ml = kind == "ml"
        dk = cfg["ML_DK"] if ml else cfg["RT_DK"]
        dv = cfg["ML_DV"] if ml else cfg["RT_DV"]
        if ml:
            W = f(ml_w_in[0])
            QK, V = H * dk, H * dv
            wq = np.stack([_tile_k(W[:, hh * dk:(hh + 1) * dk], KC) for hh in range(H)])
            wk = np.stack([_tile_k(W[:, QK + hh * dk:QK + (hh + 1) * dk], KC) for hh in range(H)])
            wv = np.stack([_tile_k(W[:, 2 * QK + j * 256:2 * QK + (j + 1) * 256], KC) for j in range(V // 256)])
            wo = np.stack([_tile_k(W[:, 2 * QK + V + j * 256:2 * QK + V + (j + 1) * 256], KC) for j in range(V // 256)])
            wg = _tile_k(W[:, 2 * QK + 2 * V:], KC)
            Wout, hg, gn = f(ml_w_out[0]), f(ml_head_norm[0]), f(mixer_norm[0])
        else:
            W = f(rt_w_in[0])
            QK, V = H * dk, H * dv
            wq = np.stack([_tile_k(W[:, hh * dk:(hh + 1) * dk], KC) for hh in range(H)])
            wk = np.stack([_tile_k(W[:, QK + hh * dk:QK + (hh + 1) * dk], KC) for hh in range(H)])
            wv = np.stack([_tile_k(W[:, 2 * QK + j * 256:2 * QK + (j + 1) * 256], KC) for j in range(V // 256)])
            wo = np.stack([_tile_k(W[:, 2 * QK + V + j * 256:2 * QK + V + (j + 1) * 256], KC) for j in range(V // 256)])
            Wout, hg, gn = f(rt_w_out[0]), f(rt_head_norm[0]), f(mixer_norm[1])
        KK = H * dv // 128
        wout = np.ascontiguousarray(Wout.reshape(KK, 128, KC, 128).transpose(2, 1, 0, 3)).reshape(KC, 128, KK * 128)
        maps = []
        for ci in range(NC):
            b, c = divmod(ci, NSEG)
            m = {"h_in": h[ci], "wq": wq, "wk": wk, "wv": wv, "gain": _gain_cols(gn, KC), "sel": sel[ci], "caus": caus, "ident": ident,
                 "um": colmask(109 if ml else 128 - NM)}
            if ml:
                m.update({"wg": wg, "nsel": nsel[ci], "rowm": rowm,
                          "gbias": np.broadcast_to(f(ml_b_gates[0]), (128, 2 * H)).copy(),
                          "convw": np.ascontiguousarray(f(ml_conv_w[0]).reshape(4, 2 * H, 128).transpose(2, 1, 0)).reshape(128, 2 * H * 4),
                          "convb": np.ascontiguousarray(f(ml_conv_b[0]).reshape(2 * H, 128).T),
                          "lnsc": np.full((128, 1), math.log(dk ** -0.5), np.float32)})
            else:
                lg = np.log(1.0 - 2.0 ** (-5.0 - np.arange(H, dtype=np.float64)))
                t = np.arange(128, dtype=np.float64)[:, None]
                m["wtab"] = (np.exp(-(t + 1.0) * lg[None, :]) * dk ** -0.5).astype(np.float32)
                m["ebtab"] = np.exp((t + 1.0) * lg[None, :]).astype(np.float32)
                m["egtab"] = np.broadcast_to(np.exp(128.0 * lg).astype(np.float32), (128, H)).copy()
                pos = np.concatenate([np.arange(128, dtype=np.float32) - (128 - NM),
                                      NM + c * SEGL + np.arange(SEGL, dtype=np.float32)]).astype(np.float32)
                inv = (10000.0 ** (-np.arange(0, dk, 2, dtype=np.float32) / dk)).astype(np.float32)
                ang = (inv[:, None] * pos[None, :]).astype(np.float32)
                m["cos_t"] = np.cos(ang).astype(np.float32)
                m["sin_t"] = np.sin(ang).astype(np.float32)
                if full:
                    m["egseg"] = np.broadcast_to(np.exp(float(SEGL) * lg).astype(np.float32), (128, H)).copy()
            if full:
                m.update({"wo": wo, "wout": wout, "hgain": np.broadcast_to(hg, (128, H * dv)).copy(), "mseg": mseg[ci]})
                m["st_all"] = np.stack([extra["st"][b * NSEG + cp] for cp in range(NSEG - 1)])
                if ml:
                    m["ng_all"] = np.concatenate([extra["ng"][b * NSEG + cp] for cp in range(NSEG - 1)], axis=1)
            maps.append(m)
        return maps

    def mixer(kind, h):
        nc1 = build_mixer_launch(cfg, kind, False, dbgt=dbg is not None)
        r1 = _run(nc1, mixer_maps(kind, False, h, None), NC)
        if dbg is not None:
            dbg["r1_" + kind] = r1
        extra = {"st": [r["st_out"] for r in r1]}
        if kind == "ml":
            extra["ng"] = [r["ng_out"] for r in r1]
        if dbg is not None:
            dbg["extra_" + kind] = extra
        nc2 = build_mixer_launch(cfg, kind, True)
        r2 = _run(nc2, mixer_maps(kind, True, h, extra), NC)
        return [r["h_out"] for r in r2]

    def _d(k, v):
        if dbg is not None:
            dbg[k] = v
    _d("h0", h)
    h = ffn_launch(h, [0], gcat(ffn1_norm[0], ffn1_norm[0]), False)
    _d("h1", h)
    h = mixer("ml", h)
    _d("h2", h)
    h = ffn_launch(h, [1, 2], gcat(ffn2_norm[0], ffn1_norm[1], ffn1_norm[1]), False)
    _d("h4", h)
    h = mixer("rt", h)
    _d("h5", h)
    h = ffn_launch(h, [3], gcat(ffn2_norm[1], final_norm), True)
    out = np.empty((B, cfg["SEQ"], D), np.float32)
    for ci in range(NC):
        b, c = divmod(ci, NSEG)
        out[b, c * SEGL:(c + 1) * SEGL] = h[ci].reshape(D, NTOK).T[128:]
    return out
```
